# Optimizing a Trainium2 kernel written in Bass

```python
import math
import jax, jax.numpy as jnp
from jax import lax
import numpy as np

D_MODEL = 1024
BATCH = 8
SEQ = 4096
DEPTH = 1

CHUNK = 64
SB_HEADS = 8
SB_HEAD_DIM = D_MODEL // 16
SB_WIDTH = SB_HEADS * SB_HEAD_DIM
CA_HEADS = 8
CA_HEAD_DIM = D_MODEL // 16
CA_WIDTH = CA_HEADS * CA_HEAD_DIM
CA_PREV_CHUNKS = 8
CA_BAND = (CA_PREV_CHUNKS + 1) * CHUNK
REL_CLIP = 256
Q_BLOCK = 128
D_FF = 4 * D_MODEL
DEEPNORM_ALPHA = (2.0 * DEPTH) ** 0.25
DEEPNORM_BETA = (8.0 * DEPTH) ** -0.25
LN_EPS = 1e-5
IN_COLS = 3 * SB_WIDTH + 3 * CA_WIDTH + 2 * D_MODEL

kernel_name = "hybrid_stickbreak_chunkrel_deepnorm"


def _layer_norm(x, g, b):
    xf = x.astype(jnp.float32)
    mu = jnp.mean(xf, axis=-1, keepdims=True)
    var = jnp.mean(jnp.square(xf - mu), axis=-1, keepdims=True)
    y = (xf - mu) * lax.rsqrt(var + LN_EPS) * g.astype(jnp.float32) + b.astype(jnp.float32)
    return y.astype(x.dtype)


def _stick_breaking(q, k, v):
    b, s_len, h, dh = q.shape
    scale = dh ** -0.5
    qh = q.transpose(0, 2, 1, 3)
    kh = k.transpose(0, 2, 1, 3)
    vh = v.transpose(0, 2, 1, 3).astype(jnp.float32)
    outs = []
    for i in range(s_len // Q_BLOCK):
        start = i * Q_BLOCK
        end = start + Q_BLOCK
        z = jnp.einsum('bhqd,bhkd->bhqk', qh[:, :, start:end], kh[:, :, :end],
                       preferred_element_type=jnp.float32) * scale
        t_pos = start + jnp.arange(Q_BLOCK)[:, None]
        s_pos = jnp.arange(end)[None, :]
        strict = s_pos < t_pos
        log_keep = jnp.where(strict, jax.nn.log_sigmoid(-z), 0.0)
        between = lax.cumsum(log_keep, axis=3, reverse=True) - log_keep
        a = jnp.where(strict, jnp.exp(jax.nn.log_sigmoid(z) + between), 0.0)
        outs.append(jnp.einsum('bhqk,bhkd->bqhd', a, vh[:, :, :end]))
    o = jnp.concatenate(outs, axis=1)
    return o.reshape(b, s_len, h * dh).astype(q.dtype)


def _rel_index():
    i = np.arange(CHUNK)[:, None]
    kk = np.arange(CA_BAND)[None, :]
    dist = (CA_PREV_CHUNKS - kk // CHUNK) * CHUNK + i - kk % CHUNK
    return np.clip(dist, -REL_CLIP, REL_CLIP) + REL_CLIP


def _chunk_attention(q, k, v, rel_bias):
    b, s_len, h, dh = q.shape
    n_chunks = s_len // CHUNK
    scale = dh ** -0.5
    pad = ((0, 0), (CA_PREV_CHUNKS, 0), (0, 0), (0, 0), (0, 0))
    qc = q.reshape(b, n_chunks, CHUNK, h, dh)
    kc = jnp.pad(k.reshape(b, n_chunks, CHUNK, h, dh), pad)
    vc = jnp.pad(v.reshape(b, n_chunks, CHUNK, h, dh), pad)
    band = jnp.arange(n_chunks)[:, None] + jnp.arange(CA_PREV_CHUNKS + 1)[None, :]
    kb = kc[:, band].reshape(b, n_chunks, CA_BAND, h, dh)
    vb = vc[:, band].reshape(b, n_chunks, CA_BAND, h, dh).astype(jnp.float32)
    scores = jnp.einsum('bcqhd,bckhd->bhcqk', qc, kb,
                        preferred_element_type=jnp.float32) * scale
    bias = rel_bias.astype(jnp.float32)[:, _rel_index()]
    valid = jnp.repeat(band >= CA_PREV_CHUNKS, CHUNK, axis=1)
    scores = jnp.where(valid[None, None, :, None, :], scores + bias[:, None], -jnp.inf)
    p = jax.nn.softmax(scores, axis=-1)
    o = jnp.einsum('bhcqk,bckhd->bcqhd', p, vb)
    return o.reshape(b, s_len, h * dh).astype(q.dtype)


def setup_inputs(seed: int = 0) -> dict:
    key = jax.random.key(seed)
    ks = jax.random.split(key, 13)
    beta = DEEPNORM_BETA
    x = jax.random.normal(ks[0], (BATCH, SEQ, D_MODEL), jnp.float32)
    col_scale = jnp.concatenate([
        jnp.ones((2 * SB_WIDTH,), jnp.float32), jnp.full((SB_WIDTH,), beta, jnp.float32),
        jnp.ones((2 * CA_WIDTH,), jnp.float32), jnp.full((CA_WIDTH,), beta, jnp.float32),
        jnp.ones((2 * D_MODEL,), jnp.float32)])
    w_in = jax.random.normal(ks[1], (D_MODEL, IN_COLS), jnp.float32) * D_MODEL ** -0.5 * col_scale
    b_gate = 0.1 * jax.random.normal(ks[2], (2 * D_MODEL,), jnp.float32)
    w_sb_proj = jax.random.normal(ks[3], (SB_WIDTH, D_MODEL), jnp.float32) * SB_WIDTH ** -0.5 * beta
    w_ca_proj = jax.random.normal(ks[4], (CA_WIDTH, D_MODEL), jnp.float32) * CA_WIDTH ** -0.5 * beta
    rel_bias = 0.2 * jax.random.normal(ks[5], (CA_HEADS, 2 * REL_CLIP + 1), jnp.float32)
    w_out = jax.random.normal(ks[6], (D_MODEL, D_MODEL), jnp.float32) * D_MODEL ** -0.5 * beta
    ln1_g = 1.0 + 0.02 * jax.random.normal(ks[7], (D_MODEL,), jnp.float32)
    ln1_b = 0.02 * jax.random.normal(ks[8], (D_MODEL,), jnp.float32)
    w_mlp_in = jax.random.normal(ks[9], (D_MODEL, D_FF), jnp.float32) * D_MODEL ** -0.5 * beta
    w_mlp_out = jax.random.normal(ks[10], (D_FF, D_MODEL), jnp.float32) * D_FF ** -0.5 * beta
    ln2_g = 1.0 + 0.02 * jax.random.normal(ks[11], (D_MODEL,), jnp.float32)
    ln2_b = 0.02 * jax.random.normal(ks[12], (D_MODEL,), jnp.float32)
    return {"x": x, "w_in": w_in, "b_gate": b_gate, "w_sb_proj": w_sb_proj,
            "w_ca_proj": w_ca_proj, "rel_bias": rel_bias, "w_out": w_out,
            "ln1_g": ln1_g, "ln1_b": ln1_b, "w_mlp_in": w_mlp_in, "w_mlp_out": w_mlp_out,
            "ln2_g": ln2_g, "ln2_b": ln2_b}


def reference(x, w_in, b_gate, w_sb_proj, w_ca_proj, rel_bias, w_out,
              ln1_g, ln1_b, w_mlp_in, w_mlp_out, ln2_g, ln2_b):
    b, s_len, _ = x.shape
    for _layer in range(DEPTH):
        h = x @ w_in
        o = 0
        q_sb = h[..., o:o + SB_WIDTH]; o += SB_WIDTH
        k_sb = h[..., o:o + SB_WIDTH]; o += SB_WIDTH
        v_sb = h[..., o:o + SB_WIDTH]; o += SB_WIDTH
        q_ca = h[..., o:o + CA_WIDTH]; o += CA_WIDTH
        k_ca = h[..., o:o + CA_WIDTH]; o += CA_WIDTH
        v_ca = h[..., o:o + CA_WIDTH]; o += CA_WIDTH
        gate_logits = h[..., o:o + 2 * D_MODEL] + b_gate
        sb_shape = (b, s_len, SB_HEADS, SB_HEAD_DIM)
        ca_shape = (b, s_len, CA_HEADS, CA_HEAD_DIM)
        y_sb = _stick_breaking(q_sb.reshape(sb_shape), k_sb.reshape(sb_shape),
                               v_sb.reshape(sb_shape)) @ w_sb_proj
        y_ca = _chunk_attention(q_ca.reshape(ca_shape), k_ca.reshape(ca_shape),
                                v_ca.reshape(ca_shape), rel_bias) @ w_ca_proj
        gates = jax.nn.sigmoid(gate_logits.astype(jnp.float32))
        merged = (gates[..., :D_MODEL] * y_sb + gates[..., D_MODEL:] * y_ca).astype(x.dtype)
        x = _layer_norm(DEEPNORM_ALPHA * x + merged @ w_out, ln1_g, ln1_b)
        ff = jnp.square(jax.nn.relu(x @ w_mlp_in)) @ w_mlp_out
        x = _layer_norm(DEEPNORM_ALPHA * x + ff, ln2_g, ln2_b)
    return x
```

```python
import contextlib
import numpy as np
import concourse.bass as bass
import concourse.mybir as mybir
from concourse.bass_utils import run_bass_kernel_spmd

F32 = mybir.dt.float32
BF16 = mybir.dt.bfloat16
AF = mybir.ActivationFunctionType
ALU = mybir.AluOpType

P = 128
D = 1024
SEQ = 4096
TB = 512
IN_COLS = 5120
D_FF = 4096
ALPHA = float(2.0 ** 0.25)
EPS = 1e-5
NEG = -30000.0
N_CORES = 8


class Buf:
    __slots__ = ("ap", "space", "lo", "hi", "lw", "rd", "name")

    def __init__(self, ap, space, lo, hi, name=""):
        self.ap = ap
        self.space = space
        self.lo = lo
        self.hi = hi
        self.lw = None
        self.rd = []
        self.name = name


class Sched:
    ENGS = ("tensor", "vector", "scalar", "gpsimd", "sync")

    def __init__(self, nc, stack, n_dma_sems=20):
        self.nc = nc
        self.prog = {e: [] for e in self.ENGS}
        self.sems = {}
        self.cnt = {}
        for e in self.ENGS:
            self.sems[e] = stack.enter_context(nc.semaphore("s_" + e))
            self.cnt[e] = 0
        self.waited = {e: {} for e in self.ENGS}
        self.dma_pool = {}
        for q in ("gpsimd", "sync", "scalar"):
            lst = []
            for i in range(n_dma_sems):
                key = "d_%s_%d" % (q, i)
                self.sems[key] = stack.enter_context(nc.semaphore(key))
                self.cnt[key] = 0
                lst.append(key)
            self.dma_pool[q] = [lst, 0]
        self.spaces = {}
        self.nspace = 0

    def new_space(self):
        self.nspace += 1
        self.spaces[self.nspace] = []
        return self.nspace

    def buf(self, ap, space=None, lo=0, hi=1, name=""):
        if space is None:
            space = self.new_space()
        b = Buf(ap, space, lo, hi, name)
        self.spaces[space].append(b)
        return b

    def _overlaps(self, b):
        return [o for o in self.spaces[b.space] if o.lo < b.hi and b.lo < o.hi]

    def _deps(self, reads, writes):
        deps = set()
        for b in reads:
            for o in self._overlaps(b):
                if o.lw is not None:
                    deps.add(o.lw)
        for b in writes:
            for o in self._overlaps(b):
                if o.lw is not None:
                    deps.add(o.lw)
                deps.update(o.rd)
        return deps

    def _emit_waits(self, eng, deps):
        w = self.waited[eng]
        best = {}
        for (k, v) in deps:
            if k == eng and eng == "tensor":
                continue
            if w.get(k, 0) >= v:
                continue
            if best.get(k, 0) < v:
                best[k] = v
        for k, v in best.items():
            self.prog[eng].append(("wait", k, v))
            w[k] = v

    def op(self, eng, reads, writes, fname, *args, **kwargs):
        deps = self._deps(reads, writes)
        self._emit_waits(eng, deps)
        self.cnt[eng] += 1
        tag = (eng, self.cnt[eng])
        self.prog[eng].append(("op", (fname, args, kwargs), eng, 1))
        for b in reads:
            b.rd.append(tag)
        for b in writes:
            b.lw = tag
            b.rd = []
        return tag

    def dma(self, q, reads, writes, out, in_):
        deps = self._deps(reads, writes)
        pool = self.dma_pool[q]
        key = pool[0][pool[1] % len(pool[0])]
        pool[1] += 1
        if self.cnt[key] > 0:
            deps.add((key, self.cnt[key]))
        self._emit_waits(q, deps)
        self.cnt[key] += 16
        tag = (key, self.cnt[key])
        self.prog[q].append(("op", ("dma_start", (), {"out": out, "in_": in_}), key, 16))
        for b in reads:
            b.rd.append(tag)
        for b in writes:
            b.lw = tag
            b.rd = []
        return tag

    def finish(self, eng="sync"):
        deps = set()
        for k, v in self.cnt.items():
            if v > 0 and k != eng:
                deps.add((k, v))
        self._emit_waits(eng, deps)

    def replay(self, block):
        nc = self.nc
        sems = self.sems

        def run(name, e):
            for it in self.prog[name]:
                if it[0] == "wait":
                    e.wait_ge(sems[it[1]], it[2])
                else:
                    fname, args, kwargs = it[1]
                    ins = getattr(e, fname)(*args, **kwargs)
                    ins.then_inc(sems[it[2]], it[3])

        @block.tensor
        def _(e):
            run("tensor", e)

        @block.vector
        def _(e):
            run("vector", e)

        @block.scalar
        def _(e):
            run("scalar", e)

        @block.gpsimd
        def _(e):
            run("gpsimd", e)

        @block.sync
        def _(e):
            run("sync", e)


def build_program(seq=SEQ, debug_taps=()):
    nb = seq // TB
    nt = seq // P
    nc = bass.Bass("TRN2", target_bir_lowering=False)
    dt = nc.dram_tensor
    x_d = dt("x", [seq, D], F32, kind="ExternalInput").ap()
    w_in_d = dt("w_in", [D, IN_COLS], F32, kind="ExternalInput").ap()
    gb_d = dt("gbT", [P, 16], F32, kind="ExternalInput").ap()
    wsb_d = dt("w_sb_proj", [512, D], F32, kind="ExternalInput").ap()
    wca_d = dt("w_ca_proj", [512, D], F32, kind="ExternalInput").ap()
    bias_d = dt("biasT", [P, 8 * 5 * P], F32, kind="ExternalInput").ap()
    wout_d = dt("w_out", [D, D], F32, kind="ExternalInput").ap()
    lnp_d = dt("lnp", [4, D], F32, kind="ExternalInput").ap()
    w1_d = dt("w_mlp_in", [D, D_FF], F32, kind="ExternalInput").ap()
    w2_d = dt("w_mlp_out", [D_FF, D], F32, kind="ExternalInput").ap()
    out_d = dt("out", [seq, D], F32, kind="ExternalOutput").ap()
    taps = {}
    for name, shape in debug_taps:
        taps[name] = dt(name, list(shape), F32, kind="ExternalOutput").ap()

    win_v = w_in_d.rearrange("(f p) c -> p f c", p=P)
    wsb_v = wsb_d.rearrange("(f p) c -> p f c", p=P)
    wca_v = wca_d.rearrange("(f p) c -> p f c", p=P)
    wout_v = wout_d.rearrange("(f p) c -> p f c", p=P)
    w1_v = w1_d.rearrange("(f p) c -> p f c", p=P)
    w2_v = w2_d.rearrange("(f p) c -> p f c", p=P)

    with contextlib.ExitStack() as stack:
        S = Sched(nc, stack)
        sb = lambda name, shape, dtype: stack.enter_context(nc.sbuf_tensor(name, list(shape), dtype))

        ksbT_t = sb("ksbT", [P, 4 * seq], BF16)
        vsb_t = sb("vsb", [P, nt * 512], BF16)
        kcaT_t = [sb("kcaT%d" % i, [P, 4 * TB], BF16) for i in range(2)]
        vca_t = [sb("vca%d" % i, [P, 4 * 512], BF16) for i in range(2)]
        biasT_t = [sb("biasT_sb%d" % i, [P, 5 * P], BF16) for i in range(2)]
        lnp_t = sb("lnp_sb", [P, 4 * D], F32)
        gb_t = sb("gb_sb", [P, 16], F32)
        ident_t = sb("ident", [P, P], BF16)
        ntri_t = sb("ntri", [P, P], BF16)
        nones_t = sb("nones", [P, P], BF16)
        pones_t = sb("pones", [P, P], BF16)
        negbig_t = sb("negbig", [P, P], BF16)
        mask_t = sb("mask01", [P, P], BF16)
        stats_t = sb("stats", [P, 128], F32)
        NSLOT = 4
        wslot_t = [sb("wslot%d" % i, [P, 4096], BF16) for i in range(NSLOT)]
        AW = 75 * 256
        arena_t = sb("arena", [P, AW], F32)
        ps_t = stack.enter_context(nc.psum_tensor("ps", [P, 4096], F32))

        arena_space = S.new_space()

        def abuf(kib_lo, kib_hi, dtype, name):
            lo, hi = int(kib_lo * 256), int(kib_hi * 256)
            ap = arena_t[:, lo:hi]
            if dtype == BF16:
                ap = ap.bitcast(BF16)
            return S.buf(ap, arena_space, lo, hi, name)

        x1 = [abuf(4 * i, 4 + 4 * i, F32, "x1_%d" % i) for i in range(4)]
        x1T = abuf(16, 24, BF16, "x1T")
        hT = [abuf(24 + 4 * i, 28 + 4 * i, BF16, "hT%d" % i) for i in range(2)]
        relu_b = [abuf(32 + 2 * i, 34 + 2 * i, F32, "relu%d" % i) for i in range(2)]
        outt = [abuf(36 + 4 * i, 40 + 4 * i, F32, "outt%d" % i) for i in range(2)]
        xT = abuf(24, 32, BF16, "xT")
        xb = [abuf(32 + 2 * i, 34 + 2 * i, BF16, "xb%d" % i) for i in range(2)]
        qcaT = abuf(36, 40, BF16, "qcaT")
        pT_b = [abuf(40 + i, 41 + i, BF16, "pT%d" % i) for i in range(3)]
        rD = abuf(43, 45, F32, "rD")
        osbT = abuf(46, 50, BF16, "osbT")
        ocaT = abuf(50, 54, BF16, "ocaT")
        qsbT = abuf(54, 58, BF16, "qsbT")
        e_b = [abuf(58 + 2 * i, 60 + 2 * i, F32, "e%d" % i) for i in range(2)]
        spm_b = [abuf(62 + i, 63 + i, BF16, "spm%d" % i) for i in range(4)]
        at_b = [abuf(66 + i, 67 + i, BF16, "at%d" % i) for i in range(4)]
        r32 = abuf(70, 72, F32, "r32")
        r16 = [abuf(72 + i, 73 + i, BF16, "r16_%d" % i) for i in range(3)]
        mT = abuf(36, 44, BF16, "mT")
        x1b = abuf(44, 46, BF16, "x1b")
        g_b = [abuf(58 + 2 * i, 60 + 2 * i, F32, "g%d" % i) for i in range(2)]
        xres = abuf(62, 66, F32, "xres")
        ybuf = abuf(66, 70, F32, "ybuf")

        ksbT = [[S.buf(ksbT_t[:, hp * seq + b * TB: hp * seq + (b + 1) * TB]) for b in range(nb)] for hp in range(4)]
        vsb = [S.buf(vsb_t[:, t * 512:(t + 1) * 512]) for t in range(nt)]
        kcaT = [S.buf(kcaT_t[i][:, :]) for i in range(2)]
        vca = [S.buf(vca_t[i][:, :]) for i in range(2)]
        biasT = [S.buf(biasT_t[i][:, :]) for i in range(2)]
        lnp = S.buf(lnp_t[:, :])
        gb = S.buf(gb_t[:, :])
        ident = S.buf(ident_t[:, :])
        ntri = S.buf(ntri_t[:, :])
        nones = S.buf(nones_t[:, :])
        pones = S.buf(pones_t[:, :])
        negbig = S.buf(negbig_t[:, :])
        mask01 = S.buf(mask_t[:, :])
        stats = [S.buf(stats_t[:, 32 * i:32 * (i + 1)]) for i in range(4)]
        wslot = [S.buf(wslot_t[i][:, :]) for i in range(NSLOT)]
        bank = [S.buf(ps_t[:, b * 512:(b + 1) * 512]) for b in range(8)]
        bank_ap = [ps_t[:, b * 512:(b + 1) * 512] for b in range(8)]
        tbank = bank[7]
        tbank_bf = ps_t[:, 7 * 512:8 * 512].bitcast(BF16)

        S.op("gpsimd", [], [pones], "memset", pones_t[:, :], 1.0)
        S.op("gpsimd", [], [nones], "memset", nones_t[:, :], -1.0)
        S.op("gpsimd", [], [negbig], "memset", negbig_t[:, :], NEG)
        S.op("gpsimd", [pones], [ident], "affine_select", ident_t[:, :], pones_t[:, :], [[-1, P]], ALU.is_equal, 0.0,
             base=0, channel_multiplier=1)
        S.op("gpsimd", [nones], [ntri], "affine_select", ntri_t[:, :], nones_t[:, :], [[-1, P]], ALU.is_ge, 0.0,
             base=0, channel_multiplier=1)
        S.op("gpsimd", [pones], [mask01], "affine_select", mask_t[:, :], pones_t[:, :], [[1, P]], ALU.is_gt, 0.0,
             base=0, channel_multiplier=-1)
        S.dma("sync", [], [gb], gb_t[:, :], gb_d)
        for i in range(4):
            S.dma("sync", [], [lnp], lnp_t[:, i * D:(i + 1) * D], lnp_d[i:i + 1, :].partition_broadcast(P))

        def wload(i, src_ap, view):
            dst = view(wslot_t[i])
            S.dma("gpsimd", [], [wslot[i]], dst, src_ap)
            return wslot[i], dst

        v8 = lambda t: t[:, :].rearrange("p (f c) -> p f c", f=8)
        v4 = lambda t: t[:, :].rearrange("p (f c) -> p f c", f=4)

        xT3 = xT.ap.rearrange("p (f t) -> p f t", f=8)
        x1T3 = x1T.ap.rearrange("p (f t) -> p f t", f=8)
        mT3 = mT.ap.rearrange("p (f t) -> p f t", f=8)
        qsb3 = qsbT.ap.rearrange("p (h t) -> p h t", h=4)
        qca3 = qcaT.ap.rearrange("p (h t) -> p h t", h=4)
        osb3 = osbT.ap.rearrange("p (h t) -> p h t", h=4)
        oca3 = ocaT.ap.rearrange("p (h t) -> p h t", h=4)
        tb3 = tbank_bf.rearrange("p (f j) -> p f j", f=8)

        pstate = {}

        def pbank(lo=0, hi=7):
            n = pstate.get((lo, hi), 0)
            pstate[(lo, hi)] = n + 1
            b = lo + n % (hi - lo)
            return bank[b], bank_ap[b]

        def mm(out_buf, out_ap, lhs_buf, lhs_ap, rhs_buf, rhs_ap, start, stop, skip=False):
            if skip:
                S.op("tensor", [lhs_buf, rhs_buf], [out_buf], "matmul", out_ap, lhs_ap, rhs_ap, start=start, stop=stop,
                     skip_group_check=True)
            else:
                S.op("tensor", [lhs_buf, rhs_buf], [out_buf], "matmul", out_ap, lhs_ap, rhs_ap, start=start, stop=stop)

        tr_range = {"v": (0, 8)}

        def transpose_tile(src_buf, src_ap, dst_buf, dst3, ti, eng):
            tbk, tap_f32 = pbank(*tr_range["v"])
            tbf = tap_f32.bitcast(BF16)
            t3 = tbf.rearrange("p (f j) -> p f j", f=8)
            for fc in range(8):
                S.op("tensor", [src_buf, ident], [tbk], "transpose", tbf[:, fc * P:(fc + 1) * P],
                     src_ap[:, fc * P:(fc + 1) * P], ident_t[:, :])
            if eng == "scalar":
                S.op("scalar", [tbk], [dst_buf], "copy", dst3[:, :, ti * P:(ti + 1) * P], t3)
            else:
                S.op("vector", [tbk], [dst_buf], "tensor_copy", dst3[:, :, ti * P:(ti + 1) * P], t3)

        def layernorm_gen(ybuf_b, gi, dst_buf, dst_ap, st):
            y = ybuf_b.ap
            sa = st.ap
            S.op("vector", [ybuf_b], [st], "bn_stats", sa[:, 0:6], y[:, 0:512])
            yield
            S.op("vector", [ybuf_b], [st], "bn_stats", sa[:, 6:12], y[:, 512:1024])
            yield
            S.op("vector", [st], [st], "bn_aggr", sa[:, 12:14], sa[:, 0:12])
            S.op("vector", [st], [st], "tensor_scalar_add", sa[:, 14:15], sa[:, 13:14], EPS)
            yield
            S.op("scalar", [st], [st], "activation", sa[:, 15:16], sa[:, 14:15], AF.Ln)
            S.op("scalar", [st], [st], "activation", sa[:, 16:17], sa[:, 15:16], AF.Exp, scale=-0.5)
            yield
            yield
            S.op("vector", [st], [st], "tensor_scalar", sa[:, 17:18], sa[:, 12:13], sa[:, 16:17], -1.0, ALU.mult, ALU.mult)
            yield
            yield
            S.op("scalar", [ybuf_b, st], [ybuf_b], "activation", y, y, AF.Identity, bias=sa[:, 17:18], scale=sa[:, 16:17])
            yield
            yield
            yield
            S.op("vector", [ybuf_b, lnp], [ybuf_b], "tensor_tensor", y, y, lnp_t[:, gi * D:(gi + 1) * D], ALU.mult)
            yield
            S.op("vector", [ybuf_b, lnp], [dst_buf], "tensor_tensor", dst_ap, y, lnp_t[:, (gi + 1) * D:(gi + 2) * D], ALU.add)
            yield

        def layernorm(ybuf_b, gi, dst_buf, dst_ap, st):
            for _ in layernorm_gen(ybuf_b, gi, dst_buf, dst_ap, st):
                pass

        def load_xT(tb):
            for st in xT_steps(tb):
                st()

        def xT_steps(tb):
            def dma(ti):
                t = 4 * tb + ti
                xbb = xb[ti % 2]
                S.dma("gpsimd", [], [xbb], xbb.ap, x_d[t * P:(t + 1) * P, :])

            def tr(ti):
                xbb = xb[ti % 2]
                transpose_tile(xbb, xbb.ap, xT, xT3, ti, "vector")

            return [lambda: (dma(0), dma(1)), lambda: (tr(0), dma(2)), lambda: (tr(1), dma(3)), lambda: tr(2), lambda: tr(3)]

        INP_ORDER = [0, 1, 2, 4, 5, 3]

        def inp_load(i, slot):
            pi = INP_ORDER[i]
            return wload(slot, win_v[:, :, pi * 512:(pi + 1) * 512], v8)

        def in_proj_groups(tb, get_piece):
            cur = tb % 2
            groups = []

            def fm(i, evac):
                for hp in range(4):
                    def g(i=i, hp=hp, evac=evac):
                        wb, wap = get_piece(i)
                        bb, bap = pbank(0, 8)
                        for fc in range(8):
                            mm(bb, bap, wb, wap[:, fc, hp * P:(hp + 1) * P], xT, xT3[:, fc, :], fc == 0, fc == 7)
                        evac(hp, bb, bap)
                    groups.append(g)

            def tm(i, evac):
                for ti in range(4):
                    def g(i=i, ti=ti, evac=evac):
                        wb, wap = get_piece(i)
                        bb, bap = pbank(0, 8)
                        for fc in range(8):
                            mm(bb, bap, xT, xT3[:, fc, ti * P:(ti + 1) * P], wb, wap[:, fc, :], fc == 0, fc == 7)
                        evac(ti, bb, bap)
                    groups.append(g)

            def ev_q(dst_buf, dst3):
                def f(hp, bb, bap):
                    S.op("vector", [bb], [dst_buf], "tensor_scalar_mul", dst3[:, hp, :], bap, 0.125)
                return f

            def ev_ksb(hp, bb, bap):
                dst = ksbT[hp][tb]
                S.op("scalar", [bb], [dst], "copy", dst.ap, bap)

            def ev_vsb(ti, bb, bap):
                dst = vsb[4 * tb + ti]
                S.op("vector", [bb], [dst], "tensor_copy", dst.ap, bap)

            def ev_kca(hp, bb, bap):
                S.op("scalar", [bb], [kcaT[cur]], "copy", kcaT_t[cur][:, hp * TB:(hp + 1) * TB], bap)

            def ev_vca(ti, bb, bap):
                S.op("vector", [bb], [vca[cur]], "tensor_copy", vca_t[cur][:, ti * 512:(ti + 1) * 512], bap)

            fm(0, ev_q(qsbT, qsb3))
            fm(1, ev_ksb)
            tm(2, ev_vsb)
            fm(3, ev_kca)
            tm(4, ev_vca)
            fm(5, ev_q(qcaT, qca3))
            return groups

        def ca_thread(tb):
            cur, prv = tb % 2, (tb + 1) % 2
            kts = [kt for kt in range(4 * tb - 4, 4 * tb + 4) if kt >= 0]
            items = [(h, kt) for h in range(8) for kt in kts]
            info = {}

            def load_bias(h):
                bt, bb = biasT_t[h % 2], biasT[h % 2]
                S.dma("gpsimd", [], [bb], bt[:, :], bias_d[:, h * 5 * P:(h + 1) * 5 * P])
                S.op("gpsimd", [], [bb], "memset", bt[0:64, 576:640], NEG)
                S.op("gpsimd", [], [bb], "memset", bt[64:128, 0:64], NEG)

            def geom(kt):
                a_ = max(kt, 4 * tb) - 4 * tb
                b_ = min(kt + 4, 4 * tb + 3) - 4 * tb
                n = b_ - a_ + 1
                t_a = kt - (4 * tb + a_) + 4
                u0 = 512 - 128 * t_a
                which = cur if kt >= 4 * tb else prv
                return slice(a_ * P, (b_ + 1) * P), n, u0, which, kt % 4

            def front(k):
                h, kt = items[k]
                hp, half = h // 2, h % 2
                pr = slice(half * 64, half * 64 + 64)
                if kt == kts[0] and h + 1 < 8:
                    load_bias(h + 1)
                cs, n, u0, which, kl = geom(kt)
                pb, pap = pbank(4, 6)
                mm(pb, pap[:, cs], ident, ident_t[:, :], biasT[h % 2], biasT_t[h % 2][:, u0:u0 + n * P], True, False)
                yield
                mm(pb, pap[:, cs], kcaT[which], kcaT_t[which][pr, hp * TB + kl * P: hp * TB + (kl + 1) * P],
                   qcaT, qca3[pr, hp, cs], False, True)
                yield
                pt = pT_b[k % 3]
                S.op("scalar", [pb], [pt], "activation", pt.ap[:, cs], pap[:, cs], AF.Exp)
                info[k] = pt

            def back(k):
                h, kt = items[k]
                hp, half = h // 2, h % 2
                pr = slice(half * 64, half * 64 + 64)
                pt = info.pop(k)
                cs, n, u0, which, kl = geom(kt)
                first, last = kt == kts[0], kt == kts[-1]
                ob, oap, db, dap = bank[6], bank_ap[6], bank[7], bank_ap[7]
                mm(db, dap[:, cs], pones, pones_t[:, :], pt, pt.ap[:, cs], first, True, skip=not first)
                yield
                mm(ob, oap[:, cs], vca[which], vca_t[which][:, kl * 512 + hp * P: kl * 512 + (hp + 1) * P],
                   pt, pt.ap[:, cs], first, True, skip=not first)
                yield
                if last:
                    S.op("vector", [db], [rD], "reciprocal", rD.ap[pr, :], dap[pr, :])
                    S.op("vector", [ob, rD], [ocaT], "tensor_tensor", oca3[pr, hp, :], oap[pr, :], rD.ap[pr, :], ALU.mult)
                    yield

            load_bias(0)
            for k in range(len(items) + 1):
                if k < len(items):
                    yield from front(k)
                if k >= 1:
                    yield from back(k - 1)

        def sb_thread(tb):
            for h in range(8):
                hp, half = h // 2, h % 2
                pr = slice(half * 64, half * 64 + 64)
                ob, oap = bank[3], bank_ap[3]
                S.op("vector", [], [r32], "memset", r32.ap, 0.0)
                steps = list(range(4 * tb + 3, -1, -1))
                ns = len(steps)
                st_info = {}

                def stageA(s):
                    kb = steps[s]
                    c0 = max(0, kb - 4 * tb) * P
                    diag = kb >= 4 * tb
                    pb, pap = pbank(0, 3)
                    eb = e_b[s % 2]
                    sp = spm_b[s % 4]
                    kbuf = ksbT[hp][kb // 4]
                    kap = ksbT_t[pr, hp * seq + kb * P: hp * seq + (kb + 1) * P]
                    mm(pb, pap[:, c0:512], kbuf, kap, qsbT, qsb3[pr, hp, c0:512], True, True)
                    S.op("scalar", [pb], [eb], "activation", eb.ap[:, c0:512], pap[:, c0:512], AF.Exp)
                    S.op("scalar", [eb], [sp], "activation", sp.ap[:, c0:512], eb.ap[:, c0:512], AF.Ln, bias=1.0)
                    if diag:
                        S.op("vector", [sp, mask01], [sp], "tensor_tensor", sp.ap[:, c0:c0 + P], sp.ap[:, c0:c0 + P], mask_t[:, :], ALU.mult)
                    st_info[s] = (kb, c0, diag, pb, pap, sp)
                    if s + 1 < ns:
                        rn = r16[(s + 1) % 3]
                        S.op("vector", [r32, sp], [r32], "tensor_tensor", r32.ap[:, c0:512], r32.ap[:, c0:512], sp.ap[:, c0:512], ALU.add)
                        S.op("vector", [r32], [rn], "tensor_copy", rn.ap[:, c0:512], r32.ap[:, c0:512])

                def stageB(s):
                    kb, c0, diag, pb, pap, sp = st_info[s]
                    c1 = c0 + P if diag else 0
                    has_carry = (s > 0) and c1 < 512
                    mm(pb, pap[:, c0:512], ntri, ntri_t[:, :], sp, sp.ap[:, c0:512], False, True, skip=True)
                    if has_carry:
                        rv = r16[s % 3]
                        mm(pb, pap[:, c1:512], nones, nones_t[:, :], rv, rv.ap[:, c1:512], False, True, skip=True)
                    ab = at_b[s % 4]
                    S.op("scalar", [pb], [ab], "activation", ab.ap[:, c0:512], pap[:, c0:512], AF.Exp)
                    if diag:
                        S.op("vector", [ab, mask01], [ab], "tensor_tensor", ab.ap[:, c0:c0 + P], ab.ap[:, c0:c0 + P], mask_t[:, :], ALU.mult)

                def stageC(s):
                    kb, c0, diag, pb, pap, sp = st_info[s]
                    ab = at_b[s % 4]
                    vb = vsb[kb]
                    mm(ob, oap[:, c0:512], vb, vb.ap[:, hp * P:(hp + 1) * P], ab, ab.ap[:, c0:512], s == 0, True, skip=(s > 0))

                for it in range(ns + 3):
                    if it < ns:
                        stageA(it)
                        yield
                    if 0 <= it - 1 < ns:
                        stageB(it - 1)
                        yield
                    if 0 <= it - 3 < ns:
                        stageC(it - 3)
                        yield
                S.op("vector", [ob], [osbT], "tensor_copy", osb3[pr, hp, :], oap[pr, :])
                yield

        def mlp_thread(tb, on_weights_done=None):
            w1 = {}
            w2 = {}

            def ld1(q):
                w1[q] = wload(2 * (q % 2), w1_v[:, :, q * 512:(q + 1) * 512], v8)

            def ld2(q):
                w2[q] = wload(2 * (q % 2) + 1, w2_v[:, 4 * q:4 * q + 4, :], v4)

            ld1(0)
            ld2(0)
            ld1(1)
            yield
            lag = []

            def flush():
                while lag:
                    lag.pop(0)()

            for j in range(9):
                if j < 8:
                    w1b, w1a = w1[j]
                    hb = hT[j % 2]
                    h3 = hb.ap.rearrange("p (c t) -> p c t", c=4)
                    for hc in range(4):
                        pb, pap = pbank(4, 8)
                        for kc in range(8):
                            mm(pb, pap, w1b, w1a[:, kc, hc * P:(hc + 1) * P], x1T, x1T3[:, kc, :], kc == 0, kc == 7)
                            if kc % 2 == 1:
                                yield
                        flush()
                        rb = relu_b[hc % 2]

                        def cons(pb=pb, pap=pap, rb=rb, hb=hb, h3=h3, hc=hc):
                            S.op("scalar", [pb], [rb], "activation", rb.ap, pap, AF.Relu)
                            S.op("scalar", [rb], [hb], "activation", h3[:, hc, :], rb.ap, AF.Square)
                        lag.append(cons)
                if j >= 1:
                    q = j - 1
                    if j < 8:
                        ld2(j)
                    if j + 1 < 8:
                        ld1(j + 1)
                    if j == 8 and on_weights_done is not None:
                        pass
                    w2b, w2a = w2[q]
                    hq = hT[q % 2]
                    hq3 = hq.ap.rearrange("p (c t) -> p c t", c=4)
                    for ti in range(4):
                        for hf in range(2):
                            pb, pap = pbank(4, 8)
                            for hc in range(4):
                                mm(pb, pap, hq, hq3[:, hc, ti * P:(ti + 1) * P], w2b, w2a[:, hc, hf * 512:(hf + 1) * 512], hc == 0, hc == 3)
                                if hc % 2 == 1:
                                    yield
                            flush()
                            xa = x1[ti].ap[:, hf * 512:(hf + 1) * 512]

                            def cons2(pb=pb, pap=pap, xa=xa, ti=ti, q=q):
                                if q == 0:
                                    S.op("vector", [x1[ti], pb], [x1[ti]], "scalar_tensor_tensor", xa, xa, ALPHA, pap, ALU.mult, ALU.add)
                                else:
                                    S.op("vector", [x1[ti], pb], [x1[ti]], "tensor_tensor", xa, xa, pap, ALU.add)
                            lag.append(cons2)
            flush()
            if on_weights_done is not None:
                on_weights_done()
            yield
            def ln_tile(ti):
                t = 4 * tb + ti
                ot = outt[ti % 2]
                yield from layernorm_gen(x1[ti], 2, ot, ot.ap, stats[2 + ti % 2])
                yield
                yield
                S.dma("sync", [ot], [], out_d[t * P:(t + 1) * P, :], ot.ap)
                yield

            for pair in ((0, 1), (2, 3)):
                ga, gb2 = ln_tile(pair[0]), ln_tile(pair[1])
                alive = [ga, gb2]
                while alive:
                    for g in list(alive):
                        try:
                            next(g)
                        except StopIteration:
                            alive.remove(g)
                    yield

        def phase_b_weights():
            return {"wsb": wload(0, wsb_v, v4), "wca": wload(1, wca_v, v4),
                    0: wload(2, win_v[:, :, 3072 + 0:3072 + 512], v8),
                    2: wload(3, win_v[:, :, 3072 + 1024:3072 + 1536], v8)}

        def phase_b(tb, pre, own_steps):
            while own_steps:
                own_steps.pop(0)()
            nsteps = xT_steps(tb + 1) if tb + 1 < nb else []
            if nsteps:
                nsteps.pop(0)()
            if pre is None:
                pre = phase_b_weights()
            wsb_b, wsb_a = pre["wsb"]
            wca_b, wca_a = pre["wca"]
            gp = {0: pre[0], 2: pre[2]}
            for fc in range(8):
                if fc == 4:
                    gp[1] = wload(2, win_v[:, :, 3072 + 512:3072 + 1024], v8)
                    gp[3] = wload(3, win_v[:, :, 3072 + 1536:3072 + 2048], v8)
                g1w = gp[0] if fc < 4 else gp[1]
                g2w = gp[2] if fc < 4 else gp[3]
                fl = fc % 4
                ysb_b, ysb_a = pbank(0, 8)
                for kc in range(4):
                    mm(ysb_b, ysb_a, wsb_b, wsb_a[:, kc, fc * P:(fc + 1) * P], osbT, osb3[:, kc, :], kc == 0, kc == 3)
                yca_b, yca_a = pbank(0, 8)
                for kc in range(4):
                    mm(yca_b, yca_a, wca_b, wca_a[:, kc, fc * P:(fc + 1) * P], ocaT, oca3[:, kc, :], kc == 0, kc == 3)
                for gi, gw in ((0, g1w), (1, g2w)):
                    gbk, gap = pbank(0, 8)
                    for kc in range(8):
                        mm(gbk, gap, gw[0], gw[1][:, kc, fl * P:(fl + 1) * P], xT, xT3[:, kc, :], kc == 0, kc == 7)
                    gt = g_b[gi]
                    col = gi * 8 + fc
                    S.op("scalar", [gbk, gb], [gt], "activation", gt.ap, gap, AF.Sigmoid, bias=gb_t[:, col:col + 1])
                S.op("vector", [ysb_b, g_b[0]], [g_b[0]], "tensor_tensor", g_b[0].ap, ysb_a, g_b[0].ap, ALU.mult)
                S.op("vector", [yca_b, g_b[1]], [g_b[1]], "tensor_tensor", g_b[1].ap, yca_a, g_b[1].ap, ALU.mult)
                S.op("vector", [g_b[0], g_b[1]], [mT], "tensor_tensor", mT3[:, fc, :], g_b[0].ap, g_b[1].ap, ALU.add)
            wo = [wload(hf, wout_v[:, :, hf * 512:(hf + 1) * 512], v8) for hf in range(2)]
            groups = []
            if tb + 1 < nb:
                nsteps.pop(0)()
                nsteps.pop(0)()
                loaded = {}
                wo_done = {"v": False}

                def get_piece(i):
                    for j in range(i + 2):
                        if j in loaded or j > 5:
                            continue
                        if j == 5:
                            if not wo_done["v"]:
                                continue
                            loaded[j] = inp_load(j, 0)
                        elif j <= i + 1:
                            if j >= 2 and (j - 2) not in loaded:
                                continue
                            loaded[j] = inp_load(j, 2 + j % 2)
                    if i not in loaded:
                        loaded[i] = inp_load(i, 0 if i == 5 else 2 + i % 2)
                    return loaded[i]

                get_piece(0)
                groups = in_proj_groups(tb + 1, get_piece)
            main_groups, tail_groups = groups[:20], groups[20:]
            lagged = None
            gi = 0
            for ti in range(4):
                t = 4 * tb + ti
                S.dma("sync", [], [xres], xres.ap, x_d[t * P:(t + 1) * P, :])
                ubs = []
                for hf in range(2):
                    ub, uap = pbank(0, 8)
                    for kc in range(8):
                        mm(ub, uap, mT, mT3[:, kc, ti * P:(ti + 1) * P], wo[hf][0], wo[hf][1][:, kc, :], kc == 0, kc == 7)
                    ubs.append((ub, uap))
                if lagged is not None:
                    transpose_tile(x1b, x1b.ap, x1T, x1T3, lagged, "scalar")
                if ti == 0:
                    while nsteps:
                        nsteps.pop(0)()
                for hf in range(2):
                    ub, uap = ubs[hf]
                    hs = slice(hf * 512, (hf + 1) * 512)
                    S.op("vector", [xres, ub], [ybuf], "scalar_tensor_tensor", ybuf.ap[:, hs], xres.ap[:, hs], ALPHA, uap, ALU.mult, ALU.add)
                for _ in range(5):
                    if gi < len(main_groups):
                        main_groups[gi]()
                        gi += 1
                layernorm(ybuf, 0, x1[ti], x1[ti].ap, stats[ti % 2])
                S.op("scalar", [x1[ti]], [x1b], "copy", x1b.ap, x1[ti].ap)
                lagged = ti
            while gi < len(main_groups):
                main_groups[gi]()
                gi += 1
            transpose_tile(x1b, x1b.ap, x1T, x1T3, lagged, "scalar")
            if tb + 1 < nb:
                wo_done["v"] = True
            for g in tail_groups:
                g()

        def chain(*gens):
            for g in gens:
                yield from g

        def count_units(tb):
            ca_u = 8 * (min(8, 4 * tb + 4) * 4 + 1)
            ml_u = 1 + 8 * 4 * 4 + 8 * 8 * 2 + 1 + 2 * 17
            return ca_u, ml_u

        load_xT(0)
        first = {}

        def get_piece0(i):
            for j in range(min(i + 3, 6)):
                if j not in first and (j < 4 or (j - 4) in first):
                    first[j] = inp_load(j, j % 4)
            return first[i]

        get_piece0(0)
        for g in in_proj_groups(0, get_piece0):
            g()
        for tb in range(nb):
            pre = {"v": None, "steps": None, "at": None}

            def wdone():
                pre["v"] = phase_b_weights()

            ca_u, ml_u = count_units(tb)
            if tb >= 1:
                side = chain(ca_thread(tb), mlp_thread(tb - 1, wdone))
                n_side = ca_u + ml_u
            else:
                side = ca_thread(tb)
                n_side = ca_u
            n_sb = 8 * (3 * (4 * tb + 4) + 1)
            pace = max(1, int(n_sb * 0.92))
            emitted = 0
            i = 0
            side_done = False

            def on_side_done(tb=tb):
                pre["steps"] = xT_steps(tb)
                pre["steps"].pop(0)()
                pre["at"] = i

            tr_range["v"] = (4, 8)
            for _ in sb_thread(tb):
                i += 1
                want = (i * n_side + pace - 1) // pace
                while not side_done and emitted < want:
                    try:
                        next(side)
                        emitted += 1
                    except StopIteration:
                        side_done = True
                        on_side_done()
                if pre["steps"] and pre["at"] is not None:
                    lagn = i - pre["at"]
                    if lagn == 18 and len(pre["steps"]) == 4:
                        pre["steps"].pop(0)()
                        pre["steps"].pop(0)()
                    elif lagn == 36 and len(pre["steps"]) == 2:
                        pre["steps"].pop(0)()
                        pre["steps"].pop(0)()
            if not side_done:
                for _ in side:
                    pass
                on_side_done()
            tr_range["v"] = (0, 8)
            phase_b(tb, pre["v"], pre["steps"])
        for _ in mlp_thread(nb - 1):
            pass

        S.finish("sync")
        with nc.Block() as block:
            S.replay(block)
    return nc


def _bias_layout(rel_bias):
    kk = np.arange(P)[:, None]
    u = np.arange(5 * P)[None, :]
    idx = np.clip(u - kk, -256, 256) + 256
    g = rel_bias[:, idx]
    return np.ascontiguousarray(np.transpose(g, (1, 0, 2)).reshape(P, 8 * 5 * P)).astype(np.float32)


def host_inputs(inputs, b):
    f = lambda a: np.ascontiguousarray(np.asarray(a, dtype=np.float32))
    lnp = np.stack([f(inputs["ln1_g"]), f(inputs["ln1_b"]), f(inputs["ln2_g"]), f(inputs["ln2_b"])], axis=0)
    return {
        "x": f(inputs["x"][b]),
        "w_in": f(inputs["w_in"]),
        "gbT": np.ascontiguousarray(f(inputs["b_gate"]).reshape(16, P).T),
        "w_sb_proj": f(inputs["w_sb_proj"]),
        "w_ca_proj": f(inputs["w_ca_proj"]),
        "biasT": _bias_layout(f(inputs["rel_bias"])),
        "w_out": f(inputs["w_out"]),
        "lnp": np.ascontiguousarray(lnp),
        "w_mlp_in": f(inputs["w_mlp_in"]),
        "w_mlp_out": f(inputs["w_mlp_out"]),
    }


_NC_CACHE = {}


def kernel(**inputs):
    x = np.asarray(inputs["x"])
    nbatch = x.shape[0]
    if "nc" not in _NC_CACHE:
        _NC_CACHE["nc"] = build_program(SEQ)
    nc = _NC_CACHE["nc"]
    in_maps = [host_inputs(inputs, b) for b in range(nbatch)]
    res = run_bass_kernel_spmd(nc, in_maps, core_ids=list(range(nbatch)))
    out = np.stack([np.asarray(r["out"], dtype=np.float32) for r in res.results], axis=0)
    return out
```

```python
import contextlib
import numpy as np
import concourse.bass as bass
import concourse.mybir as mybir
from concourse.bass_utils import run_bass_kernel_spmd

F32 = mybir.dt.float32
BF16 = mybir.dt.bfloat16
AF = mybir.ActivationFunctionType
ALU = mybir.AluOpType

P = 128
D = 1024
SEQ = 4096
TB = 512
IN_COLS = 5120
D_FF = 4096
ALPHA = float(2.0 ** 0.25)
EPS = 1e-5
NEG = -30000.0
N_CORES = 8


class Buf:
    __slots__ = ("ap", "space", "lo", "hi", "lw", "rd", "name")

    def __init__(self, ap, space, lo, hi, name=""):
        self.ap = ap
        self.space = space
        self.lo = lo
        self.hi = hi
        self.lw = None
        self.rd = []
        self.name = name


class Sched:
    ENGS = ("tensor", "vector", "scalar", "gpsimd", "sync")

    def __init__(self, nc, stack, n_dma_sems=20):
        self.nc = nc
        self.prog = {e: [] for e in self.ENGS}
        self.sems = {}
        self.cnt = {}
        for e in self.ENGS:
            self.sems[e] = stack.enter_context(nc.semaphore("s_" + e))
            self.cnt[e] = 0
        self.waited = {e: {} for e in self.ENGS}
        self.dma_pool = {}
        for q in ("gpsimd", "sync", "scalar"):
            lst = []
            for i in range(n_dma_sems):
                key = "d_%s_%d" % (q, i)
                self.sems[key] = stack.enter_context(nc.semaphore(key))
                self.cnt[key] = 0
                lst.append(key)
            self.dma_pool[q] = [lst, 0]
        self.spaces = {}
        self.nspace = 0

    def new_space(self):
        self.nspace += 1
        self.spaces[self.nspace] = []
        return self.nspace

    def buf(self, ap, space=None, lo=0, hi=1, name=""):
        if space is None:
            space = self.new_space()
        b = Buf(ap, space, lo, hi, name)
        self.spaces[space].append(b)
        return b

    def _overlaps(self, b):
        return [o for o in self.spaces[b.space] if o.lo < b.hi and b.lo < o.hi]

    def _deps(self, reads, writes):
        deps = set()
        for b in reads:
            for o in self._overlaps(b):
                if o.lw is not None:
                    deps.add(o.lw)
        for b in writes:
            for o in self._overlaps(b):
                if o.lw is not None:
                    deps.add(o.lw)
                deps.update(o.rd)
        return deps

    def _emit_waits(self, eng, deps):
        w = self.waited[eng]
        best = {}
        for (k, v) in deps:
            if k == eng and eng == "tensor":
                continue
            if w.get(k, 0) >= v:
                continue
            if best.get(k, 0) < v:
                best[k] = v
        for k, v in best.items():
            self.prog[eng].append(("wait", k, v))
            w[k] = v

    def op(self, eng, reads, writes, fname, *args, **kwargs):
        deps = self._deps(reads, writes)
        self._emit_waits(eng, deps)
        self.cnt[eng] += 1
        tag = (eng, self.cnt[eng])
        self.prog[eng].append(("op", (fname, args, kwargs), eng, 1))
        for b in reads:
            b.rd.append(tag)
        for b in writes:
            b.lw = tag
            b.rd = []
        return tag

    def dma(self, q, reads, writes, out, in_):
        deps = self._deps(reads, writes)
        pool = self.dma_pool[q]
        key = pool[0][pool[1] % len(pool[0])]
        pool[1] += 1
        if self.cnt[key] > 0:
            deps.add((key, self.cnt[key]))
        self._emit_waits(q, deps)
        self.cnt[key] += 16
        tag = (key, self.cnt[key])
        self.prog[q].append(("op", ("dma_start", (), {"out": out, "in_": in_}), key, 16))
        for b in reads:
            b.rd.append(tag)
        for b in writes:
            b.lw = tag
            b.rd = []
        return tag

    def finish(self, eng="sync"):
        deps = set()
        for k, v in self.cnt.items():
            if v > 0 and k != eng:
                deps.add((k, v))
        self._emit_waits(eng, deps)

    def replay(self, block):
        nc = self.nc
        sems = self.sems

        def run(name, e):
            for it in self.prog[name]:
                if it[0] == "wait":
                    e.wait_ge(sems[it[1]], it[2])
                else:
                    fname, args, kwargs = it[1]
                    ins = getattr(e, fname)(*args, **kwargs)
                    ins.then_inc(sems[it[2]], it[3])

        @block.tensor
        def _(e):
            run("tensor", e)

        @block.vector
        def _(e):
            run("vector", e)

        @block.scalar
        def _(e):
            run("scalar", e)

        @block.gpsimd
        def _(e):
            run("gpsimd", e)

        @block.sync
        def _(e):
            run("sync", e)


def build_program(seq=SEQ, debug_taps=()):
    nb = seq // TB
    nt = seq // P
    nc = bass.Bass("TRN2", target_bir_lowering=False)
    dt = nc.dram_tensor
    x_d = dt("x", [seq, D], F32, kind="ExternalInput").ap()
    w_in_d = dt("w_in", [D, IN_COLS], F32, kind="ExternalInput").ap()
    gb_d = dt("gbT", [P, 16], F32, kind="ExternalInput").ap()
    wsb_d = dt("w_sb_proj", [512, D], F32, kind="ExternalInput").ap()
    wca_d = dt("w_ca_proj", [512, D], F32, kind="ExternalInput").ap()
    bias_d = dt("biasT", [P, 8 * 5 * P], F32, kind="ExternalInput").ap()
    wout_d = dt("w_out", [D, D], F32, kind="ExternalInput").ap()
    lnp_d = dt("lnp", [4, D], F32, kind="ExternalInput").ap()
    w1_d = dt("w_mlp_in", [D, D_FF], F32, kind="ExternalInput").ap()
    w2_d = dt("w_mlp_out", [D_FF, D], F32, kind="ExternalInput").ap()
    out_d = dt("out", [seq, D], F32, kind="ExternalOutput").ap()
    taps = {}
    for name, shape in debug_taps:
        taps[name] = dt(name, list(shape), F32, kind="ExternalOutput").ap()

    win_v = w_in_d.rearrange("(f p) c -> p f c", p=P)
    wsb_v = wsb_d.rearrange("(f p) c -> p f c", p=P)
    wca_v = wca_d.rearrange("(f p) c -> p f c", p=P)
    wout_v = wout_d.rearrange("(f p) c -> p f c", p=P)
    w1_v = w1_d.rearrange("(f p) c -> p f c", p=P)
    w2_v = w2_d.rearrange("(f p) c -> p f c", p=P)

    with contextlib.ExitStack() as stack:
        S = Sched(nc, stack)
        sb = lambda name, shape, dtype: stack.enter_context(nc.sbuf_tensor(name, list(shape), dtype))

        ksbT_t = sb("ksbT", [P, 4 * seq], BF16)
        vsb_t = sb("vsb", [P, nt * 512], BF16)
        kcaT_t = [sb("kcaT%d" % i, [P, 4 * TB], BF16) for i in range(2)]
        vca_t = [sb("vca%d" % i, [P, 4 * 512], BF16) for i in range(2)]
        biasT_t = [sb("biasT_sb%d" % i, [P, 5 * P], BF16) for i in range(2)]
        lnp_t = sb("lnp_sb", [P, 4 * D], F32)
        gb_t = sb("gb_sb", [P, 16], F32)
        ident_t = sb("ident", [P, P], BF16)
        ntri_t = sb("ntri", [P, P], BF16)
        nones_t = sb("nones", [P, P], BF16)
        pones_t = sb("pones", [P, P], BF16)
        negbig_t = sb("negbig", [P, P], BF16)
        mask_t = sb("mask01", [P, P], BF16)
        stats_t = sb("stats", [P, 128], F32)
        NSLOT = 4
        wslot_t = [sb("wslot%d" % i, [P, 4096], BF16) for i in range(NSLOT)]
        AW = 75 * 256
        arena_t = sb("arena", [P, AW], F32)
        ps_t = stack.enter_context(nc.psum_tensor("ps", [P, 4096], F32))

        arena_space = S.new_space()

        def abuf(kib_lo, kib_hi, dtype, name):
            lo, hi = int(kib_lo * 256), int(kib_hi * 256)
            ap = arena_t[:, lo:hi]
            if dtype == BF16:
                ap = ap.bitcast(BF16)
            return S.buf(ap, arena_space, lo, hi, name)

        x1 = [abuf(4 * i, 4 + 4 * i, F32, "x1_%d" % i) for i in range(4)]
        x1T = abuf(16, 24, BF16, "x1T")
        hT = [abuf(24 + 4 * i, 28 + 4 * i, BF16, "hT%d" % i) for i in range(2)]
        relu_b = [abuf(32 + 2 * i, 34 + 2 * i, F32, "relu%d" % i) for i in range(2)]
        outt = [abuf(36 + 4 * i, 40 + 4 * i, F32, "outt%d" % i) for i in range(2)]
        xT = abuf(24, 32, BF16, "xT")
        xb = [abuf(32 + 2 * i, 34 + 2 * i, BF16, "xb%d" % i) for i in range(2)]
        qcaT = abuf(36, 40, BF16, "qcaT")
        pT_b = [abuf(40 + i, 41 + i, BF16, "pT%d" % i) for i in range(3)]
        rD = abuf(43, 45, F32, "rD")
        osbT = abuf(46, 50, BF16, "osbT")
        ocaT = abuf(50, 54, BF16, "ocaT")
        qsbT = abuf(54, 58, BF16, "qsbT")
        e_b = [abuf(58 + 2 * i, 60 + 2 * i, F32, "e%d" % i) for i in range(2)]
        spm_b = [abuf(62 + i, 63 + i, BF16, "spm%d" % i) for i in range(4)]
        at_b = [abuf(66 + i, 67 + i, BF16, "at%d" % i) for i in range(4)]
        r32 = abuf(70, 72, F32, "r32")
        r16 = [abuf(72 + i, 73 + i, BF16, "r16_%d" % i) for i in range(3)]
        mT = abuf(36, 44, BF16, "mT")
        x1b = abuf(44, 46, BF16, "x1b")
        g_b = [abuf(58 + 2 * i, 60 + 2 * i, F32, "g%d" % i) for i in range(2)]
        xres = abuf(62, 66, F32, "xres")
        ybuf = abuf(66, 70, F32, "ybuf")

        ksbT = [[S.buf(ksbT_t[:, hp * seq + b * TB: hp * seq + (b + 1) * TB]) for b in range(nb)] for hp in range(4)]
        vsb = [S.buf(vsb_t[:, t * 512:(t + 1) * 512]) for t in range(nt)]
        kcaT = [S.buf(kcaT_t[i][:, :]) for i in range(2)]
        vca = [S.buf(vca_t[i][:, :]) for i in range(2)]
        biasT = [S.buf(biasT_t[i][:, :]) for i in range(2)]
        lnp = S.buf(lnp_t[:, :])
        gb = S.buf(gb_t[:, :])
        ident = S.buf(ident_t[:, :])
        ntri = S.buf(ntri_t[:, :])
        nones = S.buf(nones_t[:, :])
        pones = S.buf(pones_t[:, :])
        negbig = S.buf(negbig_t[:, :])
        mask01 = S.buf(mask_t[:, :])
        stats = [S.buf(stats_t[:, 32 * i:32 * (i + 1)]) for i in range(4)]
        wslot = [S.buf(wslot_t[i][:, :]) for i in range(NSLOT)]
        bank = [S.buf(ps_t[:, b * 512:(b + 1) * 512]) for b in range(8)]
        bank_ap = [ps_t[:, b * 512:(b + 1) * 512] for b in range(8)]
        tbank = bank[7]
        tbank_bf = ps_t[:, 7 * 512:8 * 512].bitcast(BF16)

        S.op("gpsimd", [], [pones], "memset", pones_t[:, :], 1.0)
        S.op("gpsimd", [], [nones], "memset", nones_t[:, :], -1.0)
        S.op("gpsimd", [], [negbig], "memset", negbig_t[:, :], NEG)
        S.op("gpsimd", [pones], [ident], "affine_select", ident_t[:, :], pones_t[:, :], [[-1, P]], ALU.is_equal, 0.0,
             base=0, channel_multiplier=1)
        S.op("gpsimd", [nones], [ntri], "affine_select", ntri_t[:, :], nones_t[:, :], [[-1, P]], ALU.is_ge, 0.0,
             base=0, channel_multiplier=1)
        S.op("gpsimd", [pones], [mask01], "affine_select", mask_t[:, :], pones_t[:, :], [[1, P]], ALU.is_gt, 0.0,
             base=0, channel_multiplier=-1)
        S.dma("sync", [], [gb], gb_t[:, :], gb_d)
        for i in range(4):
            S.dma("sync", [], [lnp], lnp_t[:, i * D:(i + 1) * D], lnp_d[i:i + 1, :].partition_broadcast(P))

        def wload(i, src_ap, view):
            dst = view(wslot_t[i])
            S.dma("gpsimd", [], [wslot[i]], dst, src_ap)
            return wslot[i], dst

        v8 = lambda t: t[:, :].rearrange("p (f c) -> p f c", f=8)
        v4 = lambda t: t[:, :].rearrange("p (f c) -> p f c", f=4)

        xT3 = xT.ap.rearrange("p (f t) -> p f t", f=8)
        x1T3 = x1T.ap.rearrange("p (f t) -> p f t", f=8)
        mT3 = mT.ap.rearrange("p (f t) -> p f t", f=8)
        qsb3 = qsbT.ap.rearrange("p (h t) -> p h t", h=4)
        qca3 = qcaT.ap.rearrange("p (h t) -> p h t", h=4)
        osb3 = osbT.ap.rearrange("p (h t) -> p h t", h=4)
        oca3 = ocaT.ap.rearrange("p (h t) -> p h t", h=4)
        tb3 = tbank_bf.rearrange("p (f j) -> p f j", f=8)

        pstate = {}

        def pbank(lo=0, hi=7):
            n = pstate.get((lo, hi), 0)
            pstate[(lo, hi)] = n + 1
            b = lo + n % (hi - lo)
            return bank[b], bank_ap[b]

        def mm(out_buf, out_ap, lhs_buf, lhs_ap, rhs_buf, rhs_ap, start, stop, skip=False):
            if skip:
                S.op("tensor", [lhs_buf, rhs_buf], [out_buf], "matmul", out_ap, lhs_ap, rhs_ap, start=start, stop=stop,
                     skip_group_check=True)
            else:
                S.op("tensor", [lhs_buf, rhs_buf], [out_buf], "matmul", out_ap, lhs_ap, rhs_ap, start=start, stop=stop)

        def transpose_tile(src_buf, src_ap, dst_buf, dst3, ti, eng):
            for fc in range(8):
                S.op("tensor", [src_buf, ident], [tbank], "transpose", tbank_bf[:, fc * P:(fc + 1) * P],
                     src_ap[:, fc * P:(fc + 1) * P], ident_t[:, :])
            if eng == "scalar":
                S.op("scalar", [tbank], [dst_buf], "copy", dst3[:, :, ti * P:(ti + 1) * P], tb3)
            else:
                S.op("vector", [tbank], [dst_buf], "tensor_copy", dst3[:, :, ti * P:(ti + 1) * P], tb3)

        def layernorm_gen(ybuf_b, gi, dst_buf, dst_ap, st):
            y = ybuf_b.ap
            sa = st.ap
            S.op("vector", [ybuf_b], [st], "bn_stats", sa[:, 0:6], y[:, 0:512])
            yield
            S.op("vector", [ybuf_b], [st], "bn_stats", sa[:, 6:12], y[:, 512:1024])
            yield
            S.op("vector", [st], [st], "bn_aggr", sa[:, 12:14], sa[:, 0:12])
            S.op("vector", [st], [st], "tensor_scalar_add", sa[:, 14:15], sa[:, 13:14], EPS)
            yield
            S.op("scalar", [st], [st], "activation", sa[:, 15:16], sa[:, 14:15], AF.Ln)
            S.op("scalar", [st], [st], "activation", sa[:, 16:17], sa[:, 15:16], AF.Exp, scale=-0.5)
            yield
            yield
            S.op("vector", [st], [st], "tensor_scalar", sa[:, 17:18], sa[:, 12:13], sa[:, 16:17], -1.0, ALU.mult, ALU.mult)
            yield
            yield
            S.op("scalar", [ybuf_b, st], [ybuf_b], "activation", y, y, AF.Identity, bias=sa[:, 17:18], scale=sa[:, 16:17])
            yield
            yield
            yield
            S.op("vector", [ybuf_b, lnp], [ybuf_b], "tensor_tensor", y, y, lnp_t[:, gi * D:(gi + 1) * D], ALU.mult)
            yield
            S.op("vector", [ybuf_b, lnp], [dst_buf], "tensor_tensor", dst_ap, y, lnp_t[:, (gi + 1) * D:(gi + 2) * D], ALU.add)
            yield

        def layernorm(ybuf_b, gi, dst_buf, dst_ap, st):
            for _ in layernorm_gen(ybuf_b, gi, dst_buf, dst_ap, st):
                pass

        def load_xT(tb):
            for st in xT_steps(tb):
                st()

        def xT_steps(tb):
            def dma(ti):
                t = 4 * tb + ti
                xbb = xb[ti % 2]
                S.dma("gpsimd", [], [xbb], xbb.ap, x_d[t * P:(t + 1) * P, :])

            def tr(ti):
                xbb = xb[ti % 2]
                transpose_tile(xbb, xbb.ap, xT, xT3, ti, "vector")

            return [lambda: (dma(0), dma(1)), lambda: (tr(0), dma(2)), lambda: (tr(1), dma(3)), lambda: tr(2), lambda: tr(3)]

        INP_ORDER = [0, 1, 2, 4, 5, 3]

        def inp_load(i, slot):
            pi = INP_ORDER[i]
            return wload(slot, win_v[:, :, pi * 512:(pi + 1) * 512], v8)

        def in_proj_groups(tb, get_piece):
            cur = tb % 2
            groups = []

            def fm(i, evac):
                for hp in range(4):
                    def g(i=i, hp=hp, evac=evac):
                        wb, wap = get_piece(i)
                        bb, bap = pbank(0, 7)
                        for fc in range(8):
                            mm(bb, bap, wb, wap[:, fc, hp * P:(hp + 1) * P], xT, xT3[:, fc, :], fc == 0, fc == 7)
                        evac(hp, bb, bap)
                    groups.append(g)

            def tm(i, evac):
                for ti in range(4):
                    def g(i=i, ti=ti, evac=evac):
                        wb, wap = get_piece(i)
                        bb, bap = pbank(0, 7)
                        for fc in range(8):
                            mm(bb, bap, xT, xT3[:, fc, ti * P:(ti + 1) * P], wb, wap[:, fc, :], fc == 0, fc == 7)
                        evac(ti, bb, bap)
                    groups.append(g)

            def ev_q(dst_buf, dst3):
                def f(hp, bb, bap):
                    S.op("vector", [bb], [dst_buf], "tensor_scalar_mul", dst3[:, hp, :], bap, 0.125)
                return f

            def ev_ksb(hp, bb, bap):
                dst = ksbT[hp][tb]
                S.op("scalar", [bb], [dst], "copy", dst.ap, bap)

            def ev_vsb(ti, bb, bap):
                dst = vsb[4 * tb + ti]
                S.op("vector", [bb], [dst], "tensor_copy", dst.ap, bap)

            def ev_kca(hp, bb, bap):
                S.op("scalar", [bb], [kcaT[cur]], "copy", kcaT_t[cur][:, hp * TB:(hp + 1) * TB], bap)

            def ev_vca(ti, bb, bap):
                S.op("vector", [bb], [vca[cur]], "tensor_copy", vca_t[cur][:, ti * 512:(ti + 1) * 512], bap)

            fm(0, ev_q(qsbT, qsb3))
            fm(1, ev_ksb)
            tm(2, ev_vsb)
            fm(3, ev_kca)
            tm(4, ev_vca)
            fm(5, ev_q(qcaT, qca3))
            return groups

        def ca_thread(tb):
            cur, prv = tb % 2, (tb + 1) % 2
            items = [(h, t) for h in range(8) for t in range(5)]
            info = {}
            pending = []

            def load_bias(h):
                bt, bb = biasT_t[h % 2], biasT[h % 2]
                S.dma("gpsimd", [], [bb], bt[:, :], bias_d[:, h * 5 * P:(h + 1) * 5 * P])
                b3 = bt[:, :].rearrange("p (t i) -> p t i", t=5)
                S.op("gpsimd", [], [bb], "memset", b3[0:64, 0, 64:128], NEG)
                S.op("gpsimd", [], [bb], "memset", b3[64:128, 4, 0:64], NEG)

            def front(k):
                h, t = items[k]
                hp, half = h // 2, h % 2
                pr = slice(half * 64, half * 64 + 64)
                if t == 0 and h + 1 < 8:
                    load_bias(h + 1)
                b3 = biasT_t[h % 2][:, :].rearrange("p (t i) -> p t i", t=5)
                pb, pap = pbank(4, 6)
                brhs = b3[:, t, :].unsqueeze(1).to_broadcast([P, 4, P])
                S.op("tensor", [ident, biasT[h % 2]], [pb], "matmul", pap.rearrange("p (a i) -> p a i", a=4), ident_t[:, :], brhs,
                     start=True, stop=False)
                yield
                for ti in range(4):
                    kt = 4 * tb + ti - 4 + t
                    cs = slice(ti * P, (ti + 1) * P)
                    if kt < 0:
                        mm(pb, pap[:, cs], ident, ident_t[:, :], negbig, negbig_t[:, :], False, ti == 3)
                    else:
                        which = cur if kt >= 4 * tb else prv
                        kl = kt % 4
                        mm(pb, pap[:, cs], kcaT[which], kcaT_t[which][pr, hp * TB + kl * P: hp * TB + (kl + 1) * P],
                           qcaT, qca3[pr, hp, cs], False, ti == 3)
                    if ti % 2 == 1:
                        yield
                pt = pT_b[k % 3]
                S.op("scalar", [pb], [pt], "activation", pt.ap, pap, AF.Exp)
                info[k] = pt

            def back(k):
                h, t = items[k]
                hp, half = h // 2, h % 2
                pr = slice(half * 64, half * 64 + 64)
                pt = info.pop(k)
                ob, oap, db, dap = bank[6], bank_ap[6], bank[7], bank_ap[7]
                mm(db, dap, pones, pones_t[:, :], pt, pt.ap, t == 0, t == 4)
                yield
                for ti in range(4):
                    kt = 4 * tb + ti - 4 + t
                    cs = slice(ti * P, (ti + 1) * P)
                    if kt < 0:
                        which, kl = cur, 0
                    else:
                        which, kl = (cur if kt >= 4 * tb else prv), kt % 4
                    mm(ob, oap[:, cs], vca[which], vca_t[which][:, kl * 512 + hp * P: kl * 512 + (hp + 1) * P],
                       pt, pt.ap[:, cs], t == 0 and ti == 0, t == 4 and ti == 3)
                    if ti % 2 == 1:
                        yield
                if t == 4:
                    S.op("vector", [db], [rD], "reciprocal", rD.ap[pr, :], dap[pr, :])
                    S.op("vector", [ob, rD], [ocaT], "tensor_tensor", oca3[pr, hp, :], oap[pr, :], rD.ap[pr, :], ALU.mult)
                    yield

            load_bias(0)
            for k in range(len(items) + 1):
                if k < len(items):
                    yield from front(k)
                if k >= 1:
                    yield from back(k - 1)

        def sb_thread(tb):
            for h in range(8):
                hp, half = h // 2, h % 2
                pr = slice(half * 64, half * 64 + 64)
                ob, oap = bank[3], bank_ap[3]
                S.op("vector", [], [r32], "memset", r32.ap, 0.0)
                steps = list(range(4 * tb + 3, -1, -1))
                ns = len(steps)
                st_info = {}

                def stageA(s):
                    kb = steps[s]
                    c0 = max(0, kb - 4 * tb) * P
                    diag = kb >= 4 * tb
                    pb, pap = pbank(0, 3)
                    eb = e_b[s % 2]
                    sp = spm_b[s % 4]
                    kbuf = ksbT[hp][kb // 4]
                    kap = ksbT_t[pr, hp * seq + kb * P: hp * seq + (kb + 1) * P]
                    mm(pb, pap[:, c0:512], kbuf, kap, qsbT, qsb3[pr, hp, c0:512], True, True)
                    S.op("scalar", [pb], [eb], "activation", eb.ap[:, c0:512], pap[:, c0:512], AF.Exp)
                    S.op("scalar", [eb], [sp], "activation", sp.ap[:, c0:512], eb.ap[:, c0:512], AF.Ln, bias=1.0)
                    if diag:
                        S.op("vector", [sp, mask01], [sp], "tensor_tensor", sp.ap[:, c0:c0 + P], sp.ap[:, c0:c0 + P], mask_t[:, :], ALU.mult)
                    st_info[s] = (kb, c0, diag, pb, pap, sp)
                    if s + 1 < ns:
                        rn = r16[(s + 1) % 3]
                        S.op("vector", [r32, sp], [r32], "tensor_tensor", r32.ap[:, c0:512], r32.ap[:, c0:512], sp.ap[:, c0:512], ALU.add)
                        S.op("vector", [r32], [rn], "tensor_copy", rn.ap[:, c0:512], r32.ap[:, c0:512])

                def stageB(s):
                    kb, c0, diag, pb, pap, sp = st_info[s]
                    c1 = c0 + P if diag else 0
                    has_carry = (s > 0) and c1 < 512
                    mm(pb, pap[:, c0:512], ntri, ntri_t[:, :], sp, sp.ap[:, c0:512], False, True, skip=True)
                    if has_carry:
                        rv = r16[s % 3]
                        mm(pb, pap[:, c1:512], nones, nones_t[:, :], rv, rv.ap[:, c1:512], False, True, skip=True)
                    ab = at_b[s % 4]
                    S.op("scalar", [pb], [ab], "activation", ab.ap[:, c0:512], pap[:, c0:512], AF.Exp)
                    if diag:
                        S.op("vector", [ab, mask01], [ab], "tensor_tensor", ab.ap[:, c0:c0 + P], ab.ap[:, c0:c0 + P], mask_t[:, :], ALU.mult)

                def stageC(s):
                    kb, c0, diag, pb, pap, sp = st_info[s]
                    ab = at_b[s % 4]
                    vb = vsb[kb]
                    mm(ob, oap[:, c0:512], vb, vb.ap[:, hp * P:(hp + 1) * P], ab, ab.ap[:, c0:512], s == 0, True, skip=(s > 0))

                for it in range(ns + 3):
                    if it < ns:
                        stageA(it)
                        yield
                    if 0 <= it - 1 < ns:
                        stageB(it - 1)
                        yield
                    if 0 <= it - 3 < ns:
                        stageC(it - 3)
                        yield
                S.op("vector", [ob], [osbT], "tensor_copy", osb3[pr, hp, :], oap[pr, :])
                yield

        def mlp_thread(tb, on_weights_done=None):
            w1 = {}
            w2 = {}

            def ld1(q):
                w1[q] = wload(2 * (q % 2), w1_v[:, :, q * 512:(q + 1) * 512], v8)

            def ld2(q):
                w2[q] = wload(2 * (q % 2) + 1, w2_v[:, 4 * q:4 * q + 4, :], v4)

            ld1(0)
            ld2(0)
            ld1(1)
            yield
            lag = []

            def flush():
                while lag:
                    lag.pop(0)()

            for j in range(9):
                if j < 8:
                    w1b, w1a = w1[j]
                    hb = hT[j % 2]
                    h3 = hb.ap.rearrange("p (c t) -> p c t", c=4)
                    for hc in range(4):
                        pb, pap = pbank(4, 8)
                        for kc in range(8):
                            mm(pb, pap, w1b, w1a[:, kc, hc * P:(hc + 1) * P], x1T, x1T3[:, kc, :], kc == 0, kc == 7)
                            yield
                        flush()
                        rb = relu_b[hc % 2]

                        def cons(pb=pb, pap=pap, rb=rb, hb=hb, h3=h3, hc=hc):
                            S.op("vector", [pb], [rb], "tensor_scalar_max", rb.ap, pap, 0.0)
                            S.op("vector", [rb], [hb], "tensor_tensor", h3[:, hc, :], rb.ap, rb.ap, ALU.mult)
                        lag.append(cons)
                if j >= 1:
                    q = j - 1
                    if j < 8:
                        ld2(j)
                    if j + 1 < 8:
                        ld1(j + 1)
                    if j == 8 and on_weights_done is not None:
                        pass
                    w2b, w2a = w2[q]
                    hq = hT[q % 2]
                    hq3 = hq.ap.rearrange("p (c t) -> p c t", c=4)
                    for ti in range(4):
                        for hf in range(2):
                            pb, pap = pbank(4, 8)
                            for hc in range(4):
                                mm(pb, pap, hq, hq3[:, hc, ti * P:(ti + 1) * P], w2b, w2a[:, hc, hf * 512:(hf + 1) * 512], hc == 0, hc == 3)
                                yield
                            flush()
                            xa = x1[ti].ap[:, hf * 512:(hf + 1) * 512]

                            def cons2(pb=pb, pap=pap, xa=xa, ti=ti, q=q):
                                if q == 0:
                                    S.op("vector", [x1[ti], pb], [x1[ti]], "scalar_tensor_tensor", xa, xa, ALPHA, pap, ALU.mult, ALU.add)
                                else:
                                    S.op("vector", [x1[ti], pb], [x1[ti]], "tensor_tensor", xa, xa, pap, ALU.add)
                            lag.append(cons2)
            flush()
            if on_weights_done is not None:
                on_weights_done()
            yield
            def ln_tile(ti):
                t = 4 * tb + ti
                ot = outt[ti % 2]
                yield from layernorm_gen(x1[ti], 2, ot, ot.ap, stats[2 + ti % 2])
                yield
                yield
                S.dma("sync", [ot], [], out_d[t * P:(t + 1) * P, :], ot.ap)
                yield

            for pair in ((0, 1), (2, 3)):
                ga, gb2 = ln_tile(pair[0]), ln_tile(pair[1])
                alive = [ga, gb2]
                while alive:
                    for g in list(alive):
                        try:
                            next(g)
                        except StopIteration:
                            alive.remove(g)
                    yield

        def phase_b_weights():
            return {"wsb": wload(0, wsb_v, v4), "wca": wload(1, wca_v, v4),
                    0: wload(2, win_v[:, :, 3072 + 0:3072 + 512], v8),
                    2: wload(3, win_v[:, :, 3072 + 1024:3072 + 1536], v8)}

        def phase_b(tb, pre, own_steps):
            while own_steps:
                own_steps.pop(0)()
            nsteps = xT_steps(tb + 1) if tb + 1 < nb else []
            if nsteps:
                nsteps.pop(0)()
            if pre is None:
                pre = phase_b_weights()
            wsb_b, wsb_a = pre["wsb"]
            wca_b, wca_a = pre["wca"]
            gp = {0: pre[0], 2: pre[2]}
            for fc in range(8):
                if fc == 4:
                    gp[1] = wload(2, win_v[:, :, 3072 + 512:3072 + 1024], v8)
                    gp[3] = wload(3, win_v[:, :, 3072 + 1536:3072 + 2048], v8)
                g1w = gp[0] if fc < 4 else gp[1]
                g2w = gp[2] if fc < 4 else gp[3]
                fl = fc % 4
                ysb_b, ysb_a = pbank(0, 7)
                for kc in range(4):
                    mm(ysb_b, ysb_a, wsb_b, wsb_a[:, kc, fc * P:(fc + 1) * P], osbT, osb3[:, kc, :], kc == 0, kc == 3)
                yca_b, yca_a = pbank(0, 7)
                for kc in range(4):
                    mm(yca_b, yca_a, wca_b, wca_a[:, kc, fc * P:(fc + 1) * P], ocaT, oca3[:, kc, :], kc == 0, kc == 3)
                for gi, gw in ((0, g1w), (1, g2w)):
                    gbk, gap = pbank(0, 7)
                    for kc in range(8):
                        mm(gbk, gap, gw[0], gw[1][:, kc, fl * P:(fl + 1) * P], xT, xT3[:, kc, :], kc == 0, kc == 7)
                    gt = g_b[gi]
                    col = gi * 8 + fc
                    S.op("scalar", [gbk, gb], [gt], "activation", gt.ap, gap, AF.Sigmoid, bias=gb_t[:, col:col + 1])
                S.op("vector", [ysb_b, g_b[0]], [g_b[0]], "tensor_tensor", g_b[0].ap, ysb_a, g_b[0].ap, ALU.mult)
                S.op("vector", [yca_b, g_b[1]], [g_b[1]], "tensor_tensor", g_b[1].ap, yca_a, g_b[1].ap, ALU.mult)
                S.op("vector", [g_b[0], g_b[1]], [mT], "tensor_tensor", mT3[:, fc, :], g_b[0].ap, g_b[1].ap, ALU.add)
            wo = [wload(hf, wout_v[:, :, hf * 512:(hf + 1) * 512], v8) for hf in range(2)]
            groups = []
            if tb + 1 < nb:
                nsteps.pop(0)()
                nsteps.pop(0)()
                loaded = {}
                wo_done = {"v": False}

                def get_piece(i):
                    for j in range(i + 2):
                        if j in loaded or j > 5:
                            continue
                        if j == 5:
                            if not wo_done["v"]:
                                continue
                            loaded[j] = inp_load(j, 0)
                        elif j <= i + 1:
                            if j >= 2 and (j - 2) not in loaded:
                                continue
                            loaded[j] = inp_load(j, 2 + j % 2)
                    if i not in loaded:
                        loaded[i] = inp_load(i, 0 if i == 5 else 2 + i % 2)
                    return loaded[i]

                get_piece(0)
                groups = in_proj_groups(tb + 1, get_piece)
            main_groups, tail_groups = groups[:20], groups[20:]
            lagged = None
            gi = 0
            for ti in range(4):
                t = 4 * tb + ti
                S.dma("sync", [], [xres], xres.ap, x_d[t * P:(t + 1) * P, :])
                ubs = []
                for hf in range(2):
                    ub, uap = pbank(0, 7)
                    for kc in range(8):
                        mm(ub, uap, mT, mT3[:, kc, ti * P:(ti + 1) * P], wo[hf][0], wo[hf][1][:, kc, :], kc == 0, kc == 7)
                    ubs.append((ub, uap))
                if lagged is not None:
                    transpose_tile(x1b, x1b.ap, x1T, x1T3, lagged, "scalar")
                if ti == 0:
                    while nsteps:
                        nsteps.pop(0)()
                for hf in range(2):
                    ub, uap = ubs[hf]
                    hs = slice(hf * 512, (hf + 1) * 512)
                    S.op("vector", [xres, ub], [ybuf], "scalar_tensor_tensor", ybuf.ap[:, hs], xres.ap[:, hs], ALPHA, uap, ALU.mult, ALU.add)
                for _ in range(5):
                    if gi < len(main_groups):
                        main_groups[gi]()
                        gi += 1
                layernorm(ybuf, 0, x1[ti], x1[ti].ap, stats[ti % 2])
                S.op("scalar", [x1[ti]], [x1b], "copy", x1b.ap, x1[ti].ap)
                lagged = ti
            while gi < len(main_groups):
                main_groups[gi]()
                gi += 1
            transpose_tile(x1b, x1b.ap, x1T, x1T3, lagged, "scalar")
            if tb + 1 < nb:
                wo_done["v"] = True
            for g in tail_groups:
                g()

        def chain(*gens):
            for g in gens:
                yield from g

        def count_units(tb):
            ca_u = 40 * 3 + 40 * 3 + 8 + 1
            ml_u = 1 + 8 * 4 * 8 + 8 * 8 * 4 + 1 + 2 * 17
            return ca_u, ml_u

        load_xT(0)
        first = {}

        def get_piece0(i):
            for j in range(min(i + 3, 6)):
                if j not in first and (j < 4 or (j - 4) in first):
                    first[j] = inp_load(j, j % 4)
            return first[i]

        get_piece0(0)
        for g in in_proj_groups(0, get_piece0):
            g()
        for tb in range(nb):
            pre = {"v": None, "steps": None, "at": None}

            def wdone():
                pre["v"] = phase_b_weights()

            ca_u, ml_u = count_units(tb)
            if tb >= 1:
                side = chain(ca_thread(tb), mlp_thread(tb - 1, wdone))
                n_side = ca_u + ml_u
            else:
                side = ca_thread(tb)
                n_side = ca_u
            n_sb = 8 * (3 * (4 * tb + 4) + 1)
            pace = max(1, int(n_sb * 0.92))
            emitted = 0
            i = 0
            side_done = False

            def on_side_done(tb=tb):
                pre["steps"] = xT_steps(tb)
                pre["steps"].pop(0)()
                pre["at"] = i

            for _ in sb_thread(tb):
                i += 1
                want = (i * n_side + pace - 1) // pace
                while not side_done and emitted < want:
                    try:
                        next(side)
                        emitted += 1
                    except StopIteration:
                        side_done = True
                        on_side_done()
                if pre["steps"] and pre["at"] is not None:
                    lagn = i - pre["at"]
                    if lagn == 18 and len(pre["steps"]) == 4:
                        pre["steps"].pop(0)()
                        pre["steps"].pop(0)()
                    elif lagn == 36 and len(pre["steps"]) == 2:
                        pre["steps"].pop(0)()
                        pre["steps"].pop(0)()
            if not side_done:
                for _ in side:
                    pass
                on_side_done()
            phase_b(tb, pre["v"], pre["steps"])
        for _ in mlp_thread(nb - 1):
            pass

        S.finish("sync")
        with nc.Block() as block:
            S.replay(block)
    return nc


def _bias_layout(rel_bias):
    kk = np.arange(P)[:, None, None]
    t = np.arange(5)[None, :, None]
    i = np.arange(P)[None, None, :]
    idx = np.clip(512 - 128 * t + i - kk, -256, 256) + 256
    g = rel_bias[:, idx]
    return np.ascontiguousarray(np.transpose(g, (1, 0, 2, 3)).reshape(P, 8 * 5 * P)).astype(np.float32)


def host_inputs(inputs, b):
    f = lambda a: np.ascontiguousarray(np.asarray(a, dtype=np.float32))
    lnp = np.stack([f(inputs["ln1_g"]), f(inputs["ln1_b"]), f(inputs["ln2_g"]), f(inputs["ln2_b"])], axis=0)
    return {
        "x": f(inputs["x"][b]),
        "w_in": f(inputs["w_in"]),
        "gbT": np.ascontiguousarray(f(inputs["b_gate"]).reshape(16, P).T),
        "w_sb_proj": f(inputs["w_sb_proj"]),
        "w_ca_proj": f(inputs["w_ca_proj"]),
        "biasT": _bias_layout(f(inputs["rel_bias"])),
        "w_out": f(inputs["w_out"]),
        "lnp": np.ascontiguousarray(lnp),
        "w_mlp_in": f(inputs["w_mlp_in"]),
        "w_mlp_out": f(inputs["w_mlp_out"]),
    }


_NC_CACHE = {}


def kernel(**inputs):
    x = np.asarray(inputs["x"])
    nbatch = x.shape[0]
    if "nc" not in _NC_CACHE:
        _NC_CACHE["nc"] = build_program(SEQ)
    nc = _NC_CACHE["nc"]
    in_maps = [host_inputs(inputs, b) for b in range(nbatch)]
    res = run_bass_kernel_spmd(nc, in_maps, core_ids=list(range(nbatch)))
    out = np.stack([np.asarray(r["out"], dtype=np.float32) for r in res.results], axis=0)
    return out
```
